# Optimizing a Trainium2 kernel written in Bass

```python
import jax, jax.numpy as jnp
from jax import lax
import numpy as np

D_MODEL = 1024
BATCH = 1
SEQ = 16384
DEPTH = 4

HEAD_DIM = 64
N_MIXERS = 3
NORM_EPS = 1e-6
GRID_W = 64
A_HEADS = 16
A_KV = 4
A_G = A_HEADS // A_KV
A_WINDOW = 128
A_BLOCK = 128
B_HEADS = 16
B_KV = 4
B_G = B_HEADS // B_KV
B_BLOCK = 128
ROPE_THETA = 10000.0
C_GROUPS = ((128, 1), (512, 4), (2048, 16))
C_NGROUPS = len(C_GROUPS)
C_HEADS_PER_GROUP = 8
C_KV = 2
C_G = C_HEADS_PER_GROUP // C_KV
C_HEADS = C_NGROUPS * C_HEADS_PER_GROUP
C_BLOCK = max(w // d // 2 for w, d in C_GROUPS)

A_QW = A_HEADS * HEAD_DIM
A_KVW = A_KV * HEAD_DIM
A_OUT = A_QW
B_QW = B_HEADS * HEAD_DIM
B_KVW = B_KV * HEAD_DIM
B_OUT = B_QW
C_QW = C_HEADS * HEAD_DIM
C_KVW = C_NGROUPS * C_KV * HEAD_DIM
C_OUT = C_HEADS_PER_GROUP * HEAD_DIM
IN_WIDTH = (A_QW + 2 * A_KVW + A_OUT, B_QW + 2 * B_KVW + B_OUT, C_QW + 2 * C_KVW + C_OUT)
OUT_WIDTH = (A_OUT, B_OUT, C_OUT)

kernel_name = "hybrid_interleaved_bidir_attn_encoder"


def rms_norm(x, gain):
    xf = x.astype(jnp.float32)
    y = xf * lax.rsqrt(jnp.mean(xf * xf, axis=-1, keepdims=True) + NORM_EPS)
    return (y * gain.astype(jnp.float32)).astype(x.dtype)


def alibi_slopes(n):
    return jnp.asarray(2.0 ** (-8.0 * np.arange(1, n + 1) / n), dtype=jnp.float32)


def banded_attention(q, k, v, half, block, slopes, dist_scale, sink, length):
    n, L, kvh, g, dh = q.shape
    nb = L // block
    qb = q.reshape(n, nb, block, kvh, g, dh).astype(jnp.float32)

    def windows(t):
        tp = jnp.pad(t, ((0, 0), (block, block), (0, 0), (0, 0))).reshape(n, nb + 2, block, kvh, dh)
        return jnp.concatenate([tp[:, :-2], tp[:, 1:-1], tp[:, 2:]], axis=2).astype(jnp.float32)

    kw, vw = windows(k), windows(v)
    s = jnp.einsum("nbqkgd,nbckd->nbkgqc", qb, kw) * (dh ** -0.5)
    a = jnp.arange(block)[:, None]
    c = jnp.arange(3 * block)[None, :]
    rel = c - block - a
    kpos = jnp.arange(nb)[:, None, None] * block - block + c
    valid = (jnp.abs(rel) <= half) & (((kpos >= 0) & (kpos < length)) | (rel == 0))
    bias = -(slopes * dist_scale)[:, :, None, None] * jnp.abs(rel).astype(jnp.float32)
    s = jnp.where(valid[None, :, None, None], s + bias, -jnp.inf)
    m = jnp.max(s, axis=-1, keepdims=True)
    if sink is not None:
        sk = sink.astype(jnp.float32)[:, :, None, None]
        m = jnp.maximum(m, sk)
    p = jnp.exp(s - m)
    den = jnp.sum(p, axis=-1, keepdims=True)
    if sink is not None:
        den = den + jnp.exp(sk - m)
    o = jnp.einsum("nbkgqc,nbckd->nbqkgd", p / den, vw).reshape(n, L, kvh, g, dh)
    lse = jnp.moveaxis((m + jnp.log(den))[..., 0], 4, 2).reshape(n, L, kvh, g)
    return o.astype(q.dtype), lse


def mixer_a(h, w_in, q_gain, k_gain, sink, w_out):
    B, S, _ = h.shape
    q, k, v, gate = jnp.split(h @ w_in, [A_QW, A_QW + A_KVW, A_QW + 2 * A_KVW], axis=-1)
    q = rms_norm(q.reshape(B, S, A_KV, A_G, HEAD_DIM), q_gain)
    k = rms_norm(k.reshape(B, S, A_KV, HEAD_DIM), k_gain)
    v = v.reshape(B, S, A_KV, HEAD_DIM)
    slopes = alibi_slopes(A_HEADS).reshape(A_KV, A_G)
    o, _ = banded_attention(q, k, v, A_WINDOW, A_BLOCK, slopes, 1.0, sink.reshape(A_KV, A_G), S)
    return (o.reshape(B, S, A_OUT) * jax.nn.silu(gate)) @ w_out


def axial_rope_tables(S):
    rows = S // GRID_W
    row = jnp.repeat(jnp.arange(rows), GRID_W).astype(jnp.float32)
    col = jnp.tile(jnp.arange(GRID_W), rows).astype(jnp.float32)
    axis_dim = HEAD_DIM // 2
    freqs = 1.0 / (ROPE_THETA ** (jnp.arange(0, axis_dim, 2, dtype=jnp.float32) / axis_dim))
    ang = jnp.concatenate([row[:, None] * freqs, col[:, None] * freqs], axis=-1)
    return jnp.cos(ang), jnp.sin(ang)


def apply_rope(x, cos, sin):
    xf = x.astype(jnp.float32).reshape(*x.shape[:-1], HEAD_DIM // 2, 2)
    x1, x2 = xf[..., 0], xf[..., 1]
    out = jnp.stack([x1 * cos - x2 * sin, x1 * sin + x2 * cos], axis=-1)
    return out.reshape(x.shape).astype(x.dtype)


def dense_block_attention(q, k, v):
    B, S, kvh, g, dh = q.shape
    nb = S // B_BLOCK
    qb = jnp.moveaxis(q.reshape(B, nb, B_BLOCK, kvh, g, dh), 1, 0)
    kf, vf = k.astype(jnp.float32), v.astype(jnp.float32)

    def one_block(qblk):
        s = jnp.einsum("bqkgd,bskd->bkgqs", qblk.astype(jnp.float32), kf) * (dh ** -0.5)
        return jnp.einsum("bkgqs,bskd->bqkgd", jax.nn.softmax(s, axis=-1), vf)

    o = lax.map(one_block, qb)
    return jnp.moveaxis(o, 0, 1).reshape(B, S, kvh, g, dh).astype(q.dtype)


def mixer_b(h, w_in, q_gain, k_gain, w_out):
    B, S, _ = h.shape
    q, k, v, gate = jnp.split(h @ w_in, [B_QW, B_QW + B_KVW, B_QW + 2 * B_KVW], axis=-1)
    q = rms_norm(q.reshape(B, S, B_KV, B_G, HEAD_DIM), q_gain)
    k = rms_norm(k.reshape(B, S, B_KV, HEAD_DIM), k_gain)
    v = v.reshape(B, S, B_KV, HEAD_DIM)
    cos, sin = axial_rope_tables(S)
    q = apply_rope(q, cos[None, :, None, None, :], sin[None, :, None, None, :])
    k = apply_rope(k, cos[None, :, None, :], sin[None, :, None, :])
    o = dense_block_attention(q, k, v)
    return (o.reshape(B, S, B_OUT) * jax.nn.silu(gate)) @ w_out


def to_strided(t, dil, L, Lp):
    B = t.shape[0]
    rest = t.shape[2:]
    t = jnp.moveaxis(t.reshape(B, L, dil, *rest), 2, 1).reshape(B * dil, L, *rest)
    return jnp.pad(t, ((0, 0), (0, Lp - L)) + ((0, 0),) * len(rest))


def from_strided(t, B, dil, L):
    rest = t.shape[2:]
    t = t[:, :L].reshape(B, dil, L, *rest)
    return jnp.moveaxis(t, 1, 2).reshape(B, L * dil, *rest)


def mixer_c(h, w_in, q_gain, k_gain, w_out):
    B, S, _ = h.shape
    q, k, v, gate = jnp.split(h @ w_in, [C_QW, C_QW + C_KVW, C_QW + 2 * C_KVW], axis=-1)
    q = rms_norm(q.reshape(B, S, C_NGROUPS, C_KV, C_G, HEAD_DIM), q_gain)
    k = rms_norm(k.reshape(B, S, C_NGROUPS, C_KV, HEAD_DIM), k_gain)
    v = v.reshape(B, S, C_NGROUPS, C_KV, HEAD_DIM)
    slopes = alibi_slopes(C_HEADS).reshape(C_NGROUPS, C_KV, C_G)
    outs, lses = [], []
    for gi, (window, dil) in enumerate(C_GROUPS):
        half = window // dil // 2
        L = S // dil
        Lp = -(-L // C_BLOCK) * C_BLOCK
        qg = to_strided(q[:, :, gi], dil, L, Lp)
        kg = to_strided(k[:, :, gi], dil, L, Lp)
        vg = to_strided(v[:, :, gi], dil, L, Lp)
        o, lse = banded_attention(qg, kg, vg, half, C_BLOCK, slopes[gi], float(dil), None, L)
        outs.append(from_strided(o, B, dil, L))
        lses.append(from_strided(lse, B, dil, L))
    w = jax.nn.softmax(jnp.stack(lses, axis=0), axis=0)
    o = jnp.sum(w[..., None] * jnp.stack(outs, axis=0).astype(jnp.float32), axis=0).astype(h.dtype)
    return (o.reshape(B, S, C_OUT) * jax.nn.silu(gate)) @ w_out


def setup_inputs(seed: int = 0) -> dict:
    key = jax.random.key(seed)
    keys = jax.random.split(key, 1 + 6 * DEPTH)
    out = {"x": jax.random.normal(keys[0], (BATCH, SEQ, D_MODEL), jnp.float32)}
    for i in range(DEPTH):
        kind = i % N_MIXERS
        k = keys[1 + 6 * i: 7 + 6 * i]
        in_w, out_w = IN_WIDTH[kind], OUT_WIDTH[kind]
        out[f"l{i}_norm"] = 1.0 + 0.02 * jax.random.normal(k[0], (D_MODEL,), jnp.float32)
        out[f"l{i}_w_in"] = jax.random.normal(k[1], (D_MODEL, in_w), jnp.float32) * D_MODEL ** -0.5
        out[f"l{i}_q_gain"] = 1.0 + 0.02 * jax.random.normal(k[2], (HEAD_DIM,), jnp.float32)
        out[f"l{i}_k_gain"] = 1.0 + 0.02 * jax.random.normal(k[3], (HEAD_DIM,), jnp.float32)
        if kind == 0:
            out[f"l{i}_sink"] = 0.5 * jax.random.normal(k[4], (A_HEADS,), jnp.float32)
        out[f"l{i}_w_out"] = jax.random.normal(k[5], (out_w, D_MODEL), jnp.float32) * out_w ** -0.5
    return out


def reference(x, l0_norm, l0_w_in, l0_q_gain, l0_k_gain, l0_sink, l0_w_out,
              l1_norm, l1_w_in, l1_q_gain, l1_k_gain, l1_w_out,
              l2_norm, l2_w_in, l2_q_gain, l2_k_gain, l2_w_out,
              l3_norm, l3_w_in, l3_q_gain, l3_k_gain, l3_sink, l3_w_out):
    mixers = (mixer_a, mixer_b, mixer_c)
    layers = (
        (l0_norm, (l0_w_in, l0_q_gain, l0_k_gain, l0_sink, l0_w_out)),
        (l1_norm, (l1_w_in, l1_q_gain, l1_k_gain, l1_w_out)),
        (l2_norm, (l2_w_in, l2_q_gain, l2_k_gain, l2_w_out)),
        (l3_norm, (l3_w_in, l3_q_gain, l3_k_gain, l3_sink, l3_w_out)),
    )
    for i in range(DEPTH):
        norm, params = layers[i]
        x = x + mixers[i % N_MIXERS](rms_norm(x, norm), *params)
    return x
```

```python
import numpy as np
import ml_dtypes
from contextlib import ExitStack
import concourse.bass as bass
import concourse.mybir as mybir
from concourse.bass_utils import run_bass_kernel_spmd

F32 = mybir.dt.float32
BF16 = mybir.dt.bfloat16
AF = mybir.ActivationFunctionType
ALU = mybir.AluOpType
AX = mybir.AxisListType
NPBF = ml_dtypes.bfloat16

NCORES = 8
SEQ = 16384
TPC = 2048
NT = 16
DM = 1024
DH = 64
EPS = 1e-6
NEG = -30000.0
KIND = (0, 1, 2, 0)
CG = ((128, 1), (512, 4), (2048, 16))
FUSED = True


def lcfg(kind):
    if kind in (0, 1):
        return dict(W=2560, NH=16, NKV=4, NS=16, FO=1024)
    return dict(W=2816, NH=24, NKV=6, NS=8, FO=512)


class Sem:
    def __init__(self, h):
        self.h = h
        self.n = 0


class Buf:
    __slots__ = ("wr", "rd", "name", "prev")

    def __init__(self, name=""):
        self.wr = None
        self.rd = {}
        self.prev = []
        self.name = name


class Prog:
    def __init__(self):
        self.nc = bass.Bass("TRN2", target_bir_lowering=False)
        nc = self.nc
        self.gstack = ExitStack()
        self.eng = dict(pe=nc.tensor, act=nc.scalar, dve=nc.vector, pool=nc.gpsimd, sp=nc.sync)
        self.esem = {}
        self.waited = {k: {} for k in self.eng}
        self.dsems = []
        self.phase_id = 0
        self.in_names = []
        self.dsem_cache = {}
        self.din_cache = {}
        self.new_engine_sems()

    def sem(self, name):
        return Sem(self.gstack.enter_context(self.nc.semaphore(name)))

    def dsem(self, name):
        key = name.rsplit("_", 1)[0] if name.rsplit("_", 1)[-1].isdigit() else name
        if key in self.dsem_cache:
            return self.dsem_cache[key]
        s = self.sem(key)
        self.dsems.append(s)
        self.dsem_cache[key] = s
        return s

    def coll(self, in_ap, out_ap, sem, reads=(), writes=()):
        self._deps("pool", reads, writes)
        ins = self.nc.gpsimd.collective_compute("AllGather", ALU.bypass, replica_groups=[list(range(NCORES))],
                                                ins=[in_ap.opt()], outs=[out_ap.opt()])
        ins.then_inc(sem.h)
        sem.n += 1
        self._commit((sem, sem.n, None), reads, writes)

    def new_engine_sems(self, which=None):
        self.phase_id += 1
        for k in (which if which is not None else self.eng):
            self.esem[k] = self.sem(f"e_{k}_{self.phase_id}")

    def din(self, name, shape, dt):
        if name in self.din_cache:
            return self.din_cache[name]
        self.in_names.append(name)
        ap = self.nc.dram_tensor(name, list(shape), dt, kind="ExternalInput").ap()
        self.din_cache[name] = ap
        return ap

    def dout(self, name, shape, dt):
        return self.nc.dram_tensor(name, list(shape), dt, kind="ExternalOutput").ap()

    def dint(self, name, shape, dt):
        return self.nc.dram_tensor(name, list(shape), dt).ap()

    def _wait(self, e, dep):
        sem, val, prod = dep
        if prod == e and e == "pe":
            return
        w = self.waited[e]
        if w.get(id(sem), 0) >= val:
            return
        self.eng[e].wait_ge(sem.h, val)
        w[id(sem)] = val

    def _deps(self, e, reads, writes, skip_sem=None):
        for b in reads:
            if b.wr is not None:
                self._wait(e, b.wr)
        for b in writes:
            if b.wr is not None and skip_sem is not None and b.wr[0] is skip_sem:
                for d in getattr(b, "prev", ()):
                    self._wait(e, d)
            elif b.wr is not None:
                self._wait(e, b.wr)
            for d in b.rd.values():
                self._wait(e, d)

    def _commit(self, dep, reads, writes):
        for b in reads:
            b.rd[id(dep[0])] = dep
        for b in writes:
            if not (b.wr is not None and b.wr[0] is dep[0] and dep[2] is None and not b.rd):
                b.prev = list(b.rd.values()) + ([b.wr] if b.wr is not None else [])
            b.wr = dep
            b.rd = {}

    def op(self, e, fn, reads=(), writes=()):
        self._deps(e, reads, writes)
        ins = fn(self.eng[e])
        s = self.esem[e]
        ins.then_inc(s.h, 1)
        s.n += 1
        self._commit((s, s.n, e), reads, writes)

    def dma(self, e, out, in_, sem, reads=(), writes=()):
        self._deps(e, reads, writes, skip_sem=sem)
        ins = self.eng[e].dma_start(out=out, in_=in_)
        ins.then_inc(sem.h, 16)
        sem.n += 16
        self._commit((sem, sem.n, None), reads, writes)

    def fix_group(self, sem, bufs):
        for b in bufs:
            b.wr = (sem, sem.n, None)

    def barrier(self):
        for e in self.eng:
            for f, s in self.esem.items():
                if f != e and s.n > 0:
                    self._wait(e, (s, s.n, f))
            for s in self.dsems:
                if s.n > 0:
                    self._wait(e, (s, s.n, None))

    def finish(self, e="sp"):
        for f, s in self.esem.items():
            if f != e and s.n > 0:
                self._wait(e, (s, s.n, f))
        for s in self.dsems:
            if s.n > 0:
                self._wait(e, (s, s.n, None))


def alibi(n):
    return (2.0 ** (-8.0 * np.arange(1, n + 1) / n)).astype(np.float32)


def col_perm(kind):
    if kind == 0:
        return np.arange(2560)
    if kind == 1:
        pd = np.concatenate([np.arange(0, 64, 2), np.arange(1, 64, 2)])
        p = np.arange(2560)
        for h in range(20):
            p[h * 64:(h + 1) * 64] = h * 64 + pd
        return p
    p = []
    for gi in range(3):
        for kv in range(2):
            for g in range(4):
                p += list(((gi * 2 + kv) * 4 + g) * 64 + np.arange(64))
        for kv in range(2):
            p += list(1536 + (gi * 2 + kv) * 64 + np.arange(64))
        for kv in range(2):
            p += list(1920 + (gi * 2 + kv) * 64 + np.arange(64))
    p += list(2304 + np.arange(512))
    return np.array(p)


def host_layer_inputs(inputs, l):
    kind = KIND[l]
    w_in = np.ascontiguousarray(np.asarray(inputs[f"l{l}_w_in"], np.float32)[:, col_perm(kind)])
    qg = np.asarray(inputs[f"l{l}_q_gain"], np.float32)
    kg = np.asarray(inputs[f"l{l}_k_gain"], np.float32)
    if kind == 1:
        pd = np.concatenate([np.arange(0, 64, 2), np.arange(1, 64, 2)])
        qg, kg = qg[pd], kg[pd]
    if kind == 2:
        grow = np.concatenate([np.tile(qg, 8), np.tile(kg, 2)])
    else:
        grow = np.concatenate([np.tile(qg, 16), np.tile(kg, 4)])
    gains = np.ascontiguousarray(np.broadcast_to(grow[None, :], (128, grow.size))).astype(np.float32)
    norm = np.asarray(inputs[f"l{l}_norm"], np.float32)
    norm_pk = np.ascontiguousarray(norm.reshape(8, 128).T)
    d = {f"w_in_{l}": w_in, f"gains_{l}": gains, f"norm_{l}": norm_pk,
         f"w_out_{l}": np.ascontiguousarray(np.asarray(inputs[f"l{l}_w_out"], np.float32))}
    if kind == 0:
        d[f"sink_{l}"] = np.asarray(inputs[f"l{l}_sink"], np.float32).reshape(1, 16)
    return d


def host_consts(c):
    d = {}
    d["ident"] = np.eye(128, dtype=np.float32).astype(NPBF)
    t = np.arange(c * TPC, (c + 1) * TPC)
    row = (t // 64).astype(np.float32)
    col = (t % 64).astype(np.float32)
    freqs = (1.0 / (np.float32(10000.0) ** (np.arange(0, 32, 2, dtype=np.float32) / np.float32(32)))).astype(np.float32)
    ang = np.concatenate([row[:, None] * freqs, col[:, None] * freqs], axis=-1).astype(np.float32)
    d["rcos"] = np.cos(ang).astype(np.float32)
    d["rsin"] = np.sin(ang).astype(np.float32)
    k = np.arange(128)[:, None]
    q = np.arange(128)[None, :]
    sa = alibi(16)
    BA = np.zeros((128, 4, 4, 3, 128), np.float32)
    for o in range(3):
        rel = np.abs(128 * (o - 1) + k - q).astype(np.float32)
        for j in range(4):
            for g in range(4):
                BA[:, j, g, o, :] = np.where(rel <= 128, -sa[4 * j + g] * rel, NEG)
    d["BA"] = BA.reshape(128, -1)
    sc = alibi(24).reshape(3, 2, 4)
    BC = np.zeros((128, 3, 2, 4, 2, 128), np.float32)
    for o in range(2):
        rel = np.abs(k - q + (64 if o == 1 else -64)).astype(np.float32)
        for gi in range(3):
            for kv in range(2):
                for g in range(4):
                    BC[:, gi, kv, g, o, :] = np.where(rel <= 64, -sc[gi, kv, g] * CG[gi][1] * rel, NEG)
    d["BC"] = BC.reshape(128, -1)
    edge = np.zeros((128, 8), np.float32)
    if c == 0:
        edge[:, 1] = NEG
        edge[:64, 3] = NEG
    if c == NCORES - 1:
        edge[:, 2] = NEG
        edge[64:, 4] = NEG
    d["edge"] = edge
    return d


class G:
    pass


def setup_globals(P):
    nc = P.nc
    g = G()
    st = P.gstack
    g.x = st.enter_context(nc.sbuf_tensor("x_sb", [128, NT, DM], F32))
    g.xb = [Buf(f"x{t}") for t in range(NT)]
    g.ident = st.enter_context(nc.sbuf_tensor("ident_sb", [128, 128], BF16))
    g.identb = Buf("ident")
    g.edge = st.enter_context(nc.sbuf_tensor("edge_sb", [128, 8], F32))
    g.edgeb = Buf("edge")
    g.s_const = P.dsem("s_const")
    g.s_x = P.dsem("s_x")
    g.d_ident = P.din("ident", [128, 128], BF16)
    g.d_edge = P.din("edge", [128, 8], F32)
    P.dma("sp", g.ident[:], g.d_ident[:, :], g.s_const, writes=[g.identb])
    P.dma("sp", g.edge[:], g.d_edge[:, :], g.s_const, writes=[g.edgeb])
    P.fix_group(g.s_const, [g.identb, g.edgeb])
    return g


def load_x(P, g, d_x):
    for t4 in range(4):
        P.dma("sp" if t4 % 2 == 0 else "pool",
              g.x[:, 4 * t4:4 * t4 + 4, :],
              d_x[512 * t4:512 * t4 + 512, :].rearrange("(t p) f -> p t f", p=128),
              g.s_x, writes=g.xb[4 * t4:4 * t4 + 4])
    P.fix_group(g.s_x, g.xb)


def store_x(P, g, d_xo):
    for t4 in range(4):
        P.dma("sp" if t4 % 2 == 0 else "pool",
              d_xo[512 * t4:512 * t4 + 512, :].rearrange("(t p) f -> p t f", p=128),
              g.x[:, 4 * t4:4 * t4 + 4, :],
              g.s_x, reads=g.xb[4 * t4:4 * t4 + 4])


def phase1(P, g, l, dr):
    nc = P.nc
    kind = KIND[l]
    cf = lcfg(kind)
    W = cf["W"]
    PH = 1
    with ExitStack() as st:
        def sb(name, shape, dt):
            return st.enter_context(nc.sbuf_tensor(f"sb{PH}_{name}_{l}", list(shape), dt))

        def ps(name, shape, dt):
            return st.enter_context(nc.psum_tensor(f"ps{PH}_{name}_{l}", list(shape), dt))

        w_bf = sb("w_bf", [128, 8, W], BF16)
        w_b = [[Buf(), Buf()] for _ in range(8)]
        HW_ = W // 2
        NSTG = 2
        stage = [sb(f"wstage{i}", [128, HW_], F32) for i in range(NSTG)]
        stage_b = [Buf() for _ in range(NSTG)]
        stage_s = [P.dsem(f"s_wst{i}_{l}") for i in range(NSTG)]
        normc = sb("normc", [128, 8], F32)
        normc_b = Buf()
        nh_all = 20 if kind != 2 else 10
        gains = sb("gains", [128, nh_all * 64], F32)
        gains_b = Buf()
        s_misc = P.dsem(f"s_misc1_{l}")
        P.dma("pool", normc[:], dr["norm"][:, :], s_misc, writes=[normc_b])
        P.dma("pool", gains[:], dr["gains"][:, :], s_misc, writes=[gains_b])
        if kind == 1:
            rcos = sb("rcos", [128, NT, 32], F32)
            rsin = sb("rsin", [128, NT, 32], F32)
            rope_b = Buf()
            P.dma("pool", rcos[:], dr["rcos"].rearrange("(t p) f -> p t f", p=128), s_misc, writes=[rope_b])
            P.dma("pool", rsin[:], dr["rsin"].rearrange("(t p) f -> p t f", p=128), s_misc, writes=[rope_b])
            P.fix_group(s_misc, [rope_b])
        P.fix_group(s_misc, [normc_b, gains_b])
        i = 0
        for kc in range(8):
            for hf in range(2):
                b = i % NSTG
                P.dma("sp" if i % 2 == 0 else "pool", stage[b][:],
                      dr["w_in"][kc * 128:(kc + 1) * 128, hf * HW_:(hf + 1) * HW_],
                      stage_s[b], writes=[stage_b[b]])
                if i % 2 == 0:
                    P.op("dve", lambda e, b=b, kc=kc, hf=hf: e.tensor_scalar(
                        out=w_bf[:, kc, hf * HW_:(hf + 1) * HW_], in0=stage[b][:], scalar1=normc[:, kc:kc + 1],
                        scalar2=None, op0=ALU.mult), reads=[stage_b[b], normc_b], writes=[w_b[kc][hf]])
                else:
                    P.op("act", lambda e, b=b, kc=kc, hf=hf: e.activation(
                        out=w_bf[:, kc, hf * HW_:(hf + 1) * HW_], in_=stage[b][:], func=AF.Copy,
                        scale=normc[:, kc:kc + 1]), reads=[stage_b[b], normc_b], writes=[w_b[kc][hf]])
                i += 1
        hT = sb("hT", [128, 8, TPC], BF16)
        hT_b = [Buf() for _ in range(NT)]
        junk = sb("junk", [128, DM], BF16)
        junk_b = Buf()
        ss = sb("ss", [128, NT], F32)
        ss_b = Buf()
        rstd = sb("rstd", [128, NT], F32)
        rstd_b = Buf()
        P.op("dve", lambda e: e.memset(ss[:], 0.0), writes=[ss_b])
        for t in range(NT):
            P.op("act", lambda e, t=t: e.activation(out=junk[:], in_=g.x[:, t, :], func=AF.Square,
                                                    accum_out=ss[:, t:t + 1]),
                 reads=[g.xb[t]], writes=[junk_b, ss_b])
        P.op("dve", lambda e: e.tensor_scalar(out=rstd[:], in0=ss[:], scalar1=1.0 / DM, scalar2=EPS,
                                              op0=ALU.mult, op1=ALU.add), reads=[ss_b], writes=[rstd_b])
        P.op("act", lambda e: e.sqrt(out=rstd[:], in_=rstd[:]), reads=[rstd_b], writes=[rstd_b])
        P.op("dve", lambda e: e.reciprocal(out=rstd[:], in_=rstd[:]), reads=[rstd_b], writes=[rstd_b])
        h_bf = [sb(f"h_bf{i}", [128, DM], BF16) for i in range(2)]
        h_b = [Buf() for _ in range(2)]
        ps_T = [ps(f"ps_T{i}", [128, 1024], BF16) for i in range(2)]
        ps_T_b = [Buf() for _ in range(2)]
        for t in range(NT):
            b = t % 2
            P.op("act", lambda e, t=t, b=b: e.activation(out=h_bf[b][:], in_=g.x[:, t, :], func=AF.Copy,
                                                         scale=rstd[:, t:t + 1]),
                 reads=[g.xb[t], rstd_b], writes=[h_b[b]])
            for kc in range(8):
                P.op("pe", lambda e, b=b, kc=kc: e.transpose(ps_T[b][:, kc * 128:(kc + 1) * 128],
                                                             h_bf[b][:, kc * 128:(kc + 1) * 128], g.ident[:]),
                     reads=[h_b[b], g.identb], writes=[ps_T_b[b]])
            P.op("dve", lambda e, t=t, b=b: e.tensor_copy(out=hT[:, :, t * 128:(t + 1) * 128],
                                                          in_=ps_T[b][:, :].rearrange("p (k t) -> p k t", k=8)),
                 reads=[ps_T_b[b]], writes=[hT_b[t]])
        jobs = []
        if kind != 2:
            for t in range(NT):
                jobs.append(dict(ty="qkv", sel=slice(t * 128, (t + 1) * 128), hTb=[hT_b[t]], c0=0, nq=16, nk=4,
                                 qrow=0, krow=0, pos=t * 128, tile=t, vcol=0))
                jobs.append(dict(ty="gate", sel=slice(t * 128, (t + 1) * 128), hTb=[hT_b[t]], c0=1536, ncol=1024,
                                 pos=t * 128))
        else:
            for gi, (_, dil) in enumerate(CG):
                Ll = TPC // dil
                for r in range(dil):
                    for i_ in range(Ll // 128):
                        s_tile = r * (Ll // 128) + i_
                        u0 = i_ * 128
                        start = r + dil * u0
                        sel = slice(start, start + dil * 127 + 1, dil)
                        tl = sorted(set(tok // 128 for tok in (start, start + dil * 127)))
                        tl = list(range(tl[0], tl[-1] + 1))
                        jobs.append(dict(ty="qkv", sel=sel, hTb=[hT_b[t] for t in tl], c0=768 * gi, nq=8, nk=2,
                                         qrow=gi * 8 * 64, krow=gi * 128, pos=s_tile * 128, tile=s_tile,
                                         vcol=gi * 130))
            for t in range(NT):
                jobs.append(dict(ty="gate", sel=slice(t * 128, (t + 1) * 128), hTb=[hT_b[t]], c0=2304, ncol=512,
                                 pos=t * 128))
        ps_proj = [ps(f"ps_proj{i}", [128, 1536], F32) for i in range(2)]
        ps_proj_b = [Buf() for _ in range(2)]
        NQK = 1280 if kind != 2 else 640
        qk_f = [sb(f"qk_f{i}", [128, NQK], F32) for i in range(2)]
        qk_f_b = [Buf() for _ in range(2)]
        sq = sb("sq", [128, NQK], F32)
        ssq = sb("ssq", [128, nh_all], F32)
        ssq_b = Buf()
        qk_bf = [sb(f"qk_bf{i}", [128, NQK], BF16) for i in range(2)]
        qk_bf_b = [Buf() for _ in range(2)]
        nkv_job = 4 if kind != 2 else 2
        vaug = [sb(f"vaug{i}", [128, nkv_job, 65], BF16) for i in range(2)]
        vaug_b = [Buf() for _ in range(2)]
        vaug_s = [P.dsem(f"s_vaug{i}_{l}") for i in range(2)]
        for i in range(2):
            P.op("dve", lambda e, i=i: e.memset(vaug[i][:], 1.0), writes=[vaug_b[i]])
        g_bf = [sb(f"g_bf{i}", [128, 1024], BF16) for i in range(2)]
        g_bf_b = [Buf() for _ in range(2)]
        oT = [sb(f"oT{i}", [128, 1024], BF16) for i in range(3)]
        oT_b = [Buf() for _ in range(3)]
        oT_s = [P.dsem(f"s_oT{i}_{l}") for i in range(3)]
        if kind == 1:
            rt = [sb(f"rt{i}", [128, 20, 32], F32) for i in range(4)]
            rt_b = [Buf() for _ in range(4)]
        cnt = dict(T=0, oT=0, q=0, g=0)

        def emit_proj(job, jb):
            ncol = job["ncol"] if job["ty"] == "gate" else (job["nq"] + 2 * job["nk"]) * 64
            c0 = job["c0"]
            for cb in range((ncol + 511) // 512):
                w_ = min(512, ncol - cb * 512)
                for kc in range(8):
                    P.op("pe", lambda e, cb=cb, kc=kc, w_=w_: e.matmul(
                        ps_proj[jb][:, cb * 512:cb * 512 + w_], lhsT=hT[:, kc, job["sel"]],
                        rhs=w_bf[:, kc, c0 + cb * 512:c0 + cb * 512 + w_], start=(kc == 0), stop=(kc == 7)),
                        reads=job["hTb"] + w_b[kc], writes=[ps_proj_b[jb]])

        def emit_T(src, src_b, nchunk, dst_ap_fn, dst_buf):
            done = 0
            while done < nchunk:
                n = min(8, nchunk - done)
                tb = cnt["T"] % 2
                cnt["T"] += 1
                for c_ in range(n):
                    P.op("pe", lambda e, c_=c_, tb=tb, done=done: e.transpose(
                        ps_T[tb][:, c_ * 128:(c_ + 1) * 128], src[:, (done + c_) * 128:(done + c_ + 1) * 128],
                        g.ident[:]), reads=[src_b, g.identb], writes=[ps_T_b[tb]])
                ob = cnt["oT"] % 3
                cnt["oT"] += 1
                P.op("dve", lambda e, tb=tb, ob=ob, n=n: e.tensor_copy(out=oT[ob][:, 0:n * 128],
                                                                     in_=ps_T[tb][:, 0:n * 128]),
                     reads=[ps_T_b[tb]], writes=[oT_b[ob]])
                P.dma("pool", dst_ap_fn(done, n), oT[ob][:, 0:n * 128].rearrange("p (c t) -> p c t", t=128),
                      oT_s[ob], reads=[oT_b[ob]], writes=[dst_buf])
                done += n

        def emit_post(job, jb):
            if job["ty"] == "gate":
                ncol = job["ncol"]
                gb = cnt["g"] % 2
                cnt["g"] += 1
                P.op("act", lambda e: e.activation(out=g_bf[gb][:, 0:ncol], in_=ps_proj[jb][:, 0:ncol],
                                                   func=AF.Silu),
                     reads=[ps_proj_b[jb]], writes=[g_bf_b[gb]])
                pos = job["pos"]
                emit_T(g_bf[gb], g_bf_b[gb], ncol // 128,
                       lambda done, n: dr["GT"][done * 128:(done + n) * 128, pos:pos + 128].rearrange(
                           "(c p) t -> p c t", p=128), dr["GT_b"])
                return
            nq, nk = job["nq"], job["nk"]
            nh = nq + nk
            nqk = nh * 64
            qb = cnt["q"] % 2
            cnt["q"] += 1
            P.op("act", lambda e: e.copy(out=qk_f[qb][:, 0:nqk], in_=ps_proj[jb][:, 0:nqk]),
                 reads=[ps_proj_b[jb]], writes=[qk_f_b[qb]])
            P.op("act", lambda e: e.copy(out=vaug[qb][:, :, 0:64],
                                         in_=ps_proj[jb][:, nqk:nqk + nk * 64].rearrange("p (k d) -> p k d", d=64)),
                 reads=[ps_proj_b[jb]], writes=[vaug_b[qb]])
            if kind == 1:
                vdst = dr["EV"].rearrange("(k p) (t e) -> p k t e", p=128, e=65)[:, :, job["tile"], :]
            else:
                vdst = dr["EV"][job["pos"]:job["pos"] + 128, job["vcol"]:job["vcol"] + nk * 65].rearrange(
                    "p (k e) -> p k e", e=65)
            P.dma("sp", vdst, vaug[qb][:], vaug_s[qb], reads=[vaug_b[qb]], writes=[dr["EV_b"]])
            P.op("dve", lambda e: e.tensor_tensor(out=sq[:, 0:nqk], in0=qk_f[qb][:, 0:nqk], in1=qk_f[qb][:, 0:nqk],
                                                  op=ALU.mult), reads=[qk_f_b[qb]], writes=[ssq_b])
            P.op("dve", lambda e: e.tensor_reduce(out=ssq[:, 0:nh], in_=sq[:, 0:nqk].rearrange("p (h d) -> p h d", d=64),
                                                  axis=AX.X, op=ALU.add), reads=[ssq_b], writes=[ssq_b])
            P.op("dve", lambda e: e.tensor_scalar(out=ssq[:, 0:nh], in0=ssq[:, 0:nh], scalar1=1.0 / DH, scalar2=EPS,
                                                  op0=ALU.mult, op1=ALU.add), reads=[ssq_b], writes=[ssq_b])
            P.op("act", lambda e: e.sqrt(out=ssq[:, 0:nh], in_=ssq[:, 0:nh]), reads=[ssq_b], writes=[ssq_b])
            P.op("dve", lambda e: e.reciprocal(out=ssq[:, 0:nh], in_=ssq[:, 0:nh]), reads=[ssq_b], writes=[ssq_b])
            dstt = qk_bf[qb] if kind != 1 else qk_f[qb]
            dst_b = qk_bf_b[qb] if kind != 1 else qk_f_b[qb]
            for h in range(nh):
                P.op("dve", lambda e, h=h: e.scalar_tensor_tensor(
                    out=dstt[:, h * 64:(h + 1) * 64], in0=qk_f[qb][:, h * 64:(h + 1) * 64], scalar=ssq[:, h:h + 1],
                    in1=gains[:, h * 64:(h + 1) * 64], op0=ALU.mult, op1=ALU.mult),
                    reads=[ssq_b, qk_f_b[qb], gains_b], writes=[dst_b])
            if kind == 1:
                t = job["tile"]
                q4 = qk_f[qb][:, 0:nqk].rearrange("p (h two d) -> p h two d", two=2, d=32)
                o4 = qk_bf[qb][:, 0:nqk].rearrange("p (h two d) -> p h two d", two=2, d=32)
                cosb = rcos[:, t, :].unsqueeze(1).broadcast_to([128, nh, 32])
                sinb = rsin[:, t, :].unsqueeze(1).broadcast_to([128, nh, 32])
                x1, x2 = q4[:, :, 0, :], q4[:, :, 1, :]
                P.op("dve", lambda e: e.tensor_tensor(out=rt[0][:], in0=x1, in1=cosb, op=ALU.mult),
                     reads=[qk_f_b[qb], rope_b], writes=[rt_b[0]])
                P.op("dve", lambda e: e.tensor_tensor(out=rt[1][:], in0=x2, in1=sinb, op=ALU.mult),
                     reads=[qk_f_b[qb], rope_b], writes=[rt_b[1]])
                P.op("pool", lambda e: e.tensor_tensor(out=rt[2][:], in0=x1, in1=sinb, op=ALU.mult),
                     reads=[qk_f_b[qb], rope_b], writes=[rt_b[2]])
                P.op("pool", lambda e: e.tensor_tensor(out=rt[3][:], in0=x2, in1=cosb, op=ALU.mult),
                     reads=[qk_f_b[qb], rope_b], writes=[rt_b[3]])
                P.op("dve", lambda e: e.tensor_tensor(out=o4[:, :, 0, :], in0=rt[0][:], in1=rt[1][:], op=ALU.subtract),
                     reads=[rt_b[0], rt_b[1]], writes=[qk_bf_b[qb]])
                P.op("pool", lambda e: e.tensor_tensor(out=o4[:, :, 1, :], in0=rt[2][:], in1=rt[3][:], op=ALU.add),
                     reads=[rt_b[2], rt_b[3]], writes=[qk_bf_b[qb]])
            pos = job["pos"]
            qrow, krow = job["qrow"], job["krow"]
            nqc, nkc = nq // 2, nk // 2

            def dst(done, n):
                if done < nqc:
                    return dr["QT"][qrow + done * 128:qrow + (done + n) * 128, pos:pos + 128].rearrange(
                        "(c p) t -> p c t", p=128)
                d2 = done - nqc
                return dr["EK"][krow + d2 * 128:krow + (d2 + n) * 128, pos:pos + 128].rearrange(
                    "(c p) t -> p c t", p=128)

            emit_T(qk_bf[qb], qk_bf_b[qb], nqc, dst, dr["QT_b"])
            kview = qk_bf[qb][:, nq * 64:nqk]
            emit_T(kview, qk_bf_b[qb], nkc,
                   lambda done, n: dr["EK"][krow + done * 128:krow + (done + n) * 128, pos:pos + 128].rearrange(
                       "(c p) t -> p c t", p=128), dr["EK_b"])

        for i, job in enumerate(jobs):
            emit_proj(job, i % 2)
            if i > 0:
                emit_post(jobs[i - 1], (i - 1) % 2)
        emit_post(jobs[-1], (len(jobs) - 1) % 2)
        P.barrier()


def load_wout(P, nc, st, l, dr, NS):
    wo = st.enter_context(nc.sbuf_tensor(f"sbwo_{l}", [64, NS, DM], BF16))
    wo_b = [Buf() for _ in range(NS // 2)]
    wst = [st.enter_context(nc.sbuf_tensor(f"sbwost{i}_{l}", [64, 2, DM], F32)) for i in range(2)]
    wst_b = [Buf() for _ in range(2)]
    wst_s = [P.dsem(f"s_wost{i}_{l}") for i in range(2)]
    src = dr["w_out"].rearrange("(h d) n -> d h n", d=64)
    for i in range(NS // 2):
        b = i % 2
        P.dma("sp", wst[b][:], src[:, 2 * i:2 * i + 2, :], wst_s[b], writes=[wst_b[b]])
        if i % 2 == 0:
            P.op("act", lambda e, i=i, b=b: e.copy(out=wo[:, 2 * i:2 * i + 2, :], in_=wst[b][:]),
                 reads=[wst_b[b]], writes=[wo_b[i]])
        else:
            P.op("dve", lambda e, i=i, b=b: e.tensor_copy(out=wo[:, 2 * i:2 * i + 2, :], in_=wst[b][:]),
                 reads=[wst_b[b]], writes=[wo_b[i]])
    return wo, wo_b


def phase2_banded(P, g, l, dr):
    nc = P.nc
    kind = KIND[l]
    cf = lcfg(kind)
    NS, NKV = cf["NS"], cf["NKV"]
    PH = 2
    with ExitStack() as st:
        def sb(name, shape, dt):
            return st.enter_context(nc.sbuf_tensor(f"sb{PH}_{name}_{l}", list(shape), dt))

        def ps(name, shape, dt):
            return st.enter_context(nc.psum_tensor(f"ps{PH}_{name}_{l}", list(shape), dt))

        wo, wo_b = load_wout(P, nc, st, l, dr, NS)
        s_misc = P.dsem(f"s_misc2_{l}")
        if kind == 0:
            bias = sb("bias", [128, 6144], F32)
            bias_b = Buf()
            P.dma("pool", bias[:], dr["bias"][:, :], s_misc, writes=[bias_b])
        ones = sb("ones", [65, 64], F32)
        ones_b = Buf()
        P.op("dve", lambda e: e.memset(ones[:], 1.0), writes=[ones_b])
        if kind == 0:
            sinke = sb("sinke", [65, 16], F32)
            sinke_b = Buf()
            P.dma("pool", sinke[64:65, :], dr["sink"][:, :], s_misc, writes=[sinke_b])
            P.fix_group(s_misc, [sinke_b, bias_b])
            P.op("act", lambda e: e.activation(out=sinke[64:65, :], in_=sinke[64:65, :], func=AF.Exp),
                 reads=[sinke_b], writes=[sinke_b])
        NSW = 768 if kind == 0 else 1024
        ps_S = [ps(f"ps_S{i}", [128, 1024], F32) for i in range(2)]
        ps_S_b = [Buf() for _ in range(2)]
        ps_O = ps("ps_O", [65, 512], F32)
        ps_O_b = Buf()
        ps_bc = ps("ps_bc", [64, 512], F32)
        ps_bc_b = Buf()
        ps_y = ps("ps_y", [128, 1024], F32)
        ps_y_b = Buf()
        T = [sb(f"T{i}", [128, 1024], F32) for i in range(2)]
        T_b = [Buf() for _ in range(2)]
        Pm = [sb(f"Pm{i}", [128, 1024], BF16) for i in range(2)]
        Pm_b = [Buf() for _ in range(2)]
        nhq = 16 if kind == 0 else 8
        QTt = [sb(f"QTt{i}", [64, nhq, 128], BF16) for i in range(2)]
        QTt_b = [Buf() for _ in range(2)]
        QTt_s = [P.dsem(f"s_QTt{i}_{l}") for i in range(2)]
        GTt = [sb(f"GTt{i}", [64, NS, 128], BF16) for i in range(2)]
        GTt_b = [Buf() for _ in range(2)]
        GTt_s = [P.dsem(f"s_GTt{i}_{l}") for i in range(2)]
        OG = [sb(f"OG{i}", [64, NS, 128], BF16) for i in range(2)]
        OG_b = [Buf() for _ in range(2)]
        den = sb("den", [65, 1024], F32)
        den_b = Buf()
        tmp = sb("tmp", [64, 1024], F32)
        tmp_b = Buf()
        cnt = dict(r=0)

        if kind == 0:
            EXT = TPC + 256
            Ksb = sb("Ksb", [64, 4, EXT], BF16)
            Ksb_b = Buf()
            Vsb = sb("Vsb", [128, 18, 260], BF16)
            Vsb_b = Buf()
            s_kv = P.dsem(f"s_kvk_{l}")
            s_kv2 = P.dsem(f"s_kvv_{l}")
            P.dma("sp", Ksb[:], dr["KX"].rearrange("(k d) s -> d k s", d=64), s_kv, reads=[dr["KX_b"]], writes=[Ksb_b])
            P.dma("pool", Vsb[:], dr["VX"].rearrange("(t p) c -> p t c", p=128), s_kv2, reads=[dr["VX_b"]],
                  writes=[Vsb_b])
            QTsrc = dr["QT"].rearrange("(h d) s -> d h s", d=64)
            GTsrc = dr["GT"].rearrange("(h d) s -> d h s", d=64)
            v4 = lambda ap: ap.rearrange("p (g q) -> p g q", g=4)

            def a_loads(t):
                tb = t % 2
                P.dma("sp", QTt[tb][:], QTsrc[:, :, t * 128:(t + 1) * 128], QTt_s[tb], reads=[dr["QT_b"]],
                      writes=[QTt_b[tb]])
                P.dma("sp", GTt[tb][:], GTsrc[:, :, t * 128:(t + 1) * 128], GTt_s[tb], reads=[dr["GT_b"]],
                      writes=[GTt_b[tb]])

            def a_qk(n):
                t, i = divmod(n, 8)
                j, hp = divmod(i, 2)
                tb, rb = t % 2, n % 2
                for g2 in range(2):
                    hh = 4 * j + 2 * hp + g2
                    for o in range(3):
                        P.op("pe", lambda e, g2=g2, o=o, hh=hh: e.matmul(
                            ps_S[rb][:, (g2 * 3 + o) * 128:(g2 * 3 + o + 1) * 128],
                            lhsT=Ksb[:, j, (t + o) * 128:(t + o + 1) * 128], rhs=QTt[tb][:, hh, :],
                            start=True, stop=True), reads=[Ksb_b, QTt_b[tb]], writes=[ps_S_b[rb]])

            def a_be(n):
                t, i = divmod(n, 8)
                j, hp = divmod(i, 2)
                rb = n % 2
                boff = (j * 4 + 2 * hp) * 384
                P.op("dve", lambda e: e.scalar_tensor_tensor(
                    out=T[rb][:, 0:768], in0=ps_S[rb][:, 0:768], scalar=0.125, in1=bias[:, boff:boff + 768],
                    op0=ALU.mult, op1=ALU.add), reads=[ps_S_b[rb], bias_b], writes=[T_b[rb]])
                if t == 0 or t == NT - 1:
                    for o in range(3):
                        ecol = 1 if (t == 0 and o == 0) else (2 if (t == NT - 1 and o == 2) else 0)
                        P.op("act", lambda e, o=o, ecol=ecol: e.activation(
                            out=Pm[rb][:, 0:768].rearrange("p (g o q) -> p g o q", g=2, o=3)[:, :, o, :],
                            in_=T[rb][:, 0:768].rearrange("p (g o q) -> p g o q", g=2, o=3)[:, :, o, :],
                            func=AF.Exp, bias=g.edge[:, ecol:ecol + 1]),
                            reads=[T_b[rb], g.edgeb], writes=[Pm_b[rb]])
                else:
                    P.op("act", lambda e: e.activation(out=Pm[rb][:, 0:768], in_=T[rb][:, 0:768], func=AF.Exp),
                         reads=[T_b[rb]], writes=[Pm_b[rb]])

            def a_pv(n):
                t, i = divmod(n, 8)
                j, hp = divmod(i, 2)
                rb = n % 2
                for g2 in range(2):
                    gg = 2 * hp + g2
                    for o in range(3):
                        P.op("pe", lambda e, g2=g2, o=o, gg=gg: e.matmul(
                            ps_O[0:65, gg * 128:(gg + 1) * 128], lhsT=Vsb[:, t + o, j * 65:(j + 1) * 65],
                            rhs=Pm[rb][:, (g2 * 3 + o) * 128:(g2 * 3 + o + 1) * 128],
                            start=(o == 0), stop=(o == 2)), reads=[Vsb_b, Pm_b[rb]], writes=[ps_O_b])

            def a_fin(t, j):
                tb = t % 2
                for gg in range(4):
                    P.op("dve", lambda e, gg=gg: e.tensor_scalar(
                        out=den[64:65, gg * 128:(gg + 1) * 128], in0=ps_O[64:65, gg * 128:(gg + 1) * 128],
                        scalar1=sinke[64:65, 4 * j + gg:4 * j + gg + 1], scalar2=None, op0=ALU.add),
                        reads=[ps_O_b, sinke_b], writes=[den_b])
                P.op("dve", lambda e: e.tensor_tensor(
                    out=v4(tmp[:, 0:512]), in0=v4(ps_O[0:64, 0:512]),
                    in1=GTt[tb][:, 4 * j:4 * j + 4, :], op=ALU.mult),
                    reads=[ps_O_b, GTt_b[tb]], writes=[tmp_b])
                P.op("dve", lambda e: e.reciprocal(out=den[64:65, 0:512], in_=den[64:65, 0:512]),
                     reads=[den_b], writes=[den_b])
                P.op("pe", lambda e: e.matmul(ps_bc[0:64, 0:512], lhsT=ones[64:65, 0:64], rhs=den[64:65, 0:512],
                                              start=True, stop=True), reads=[den_b, ones_b], writes=[ps_bc_b])
                P.op("dve", lambda e: e.tensor_tensor(
                    out=OG[tb][:, 4 * j:4 * j + 4, :], in0=v4(tmp[:, 0:512]), in1=v4(ps_bc[0:64, 0:512]),
                    op=ALU.mult), reads=[tmp_b, ps_bc_b], writes=[OG_b[tb]])

            def a_outproj(t):
                tb = t % 2
                for cb in range(2):
                    for h in range(16):
                        P.op("pe", lambda e, cb=cb, h=h: e.matmul(
                            ps_y[:, cb * 512:(cb + 1) * 512], lhsT=OG[tb][:, h, :], rhs=wo[:, h, cb * 512:(cb + 1) * 512],
                            start=(h == 0), stop=(h == 15)), reads=[OG_b[tb], wo_b[h // 2]], writes=[ps_y_b])
                P.op("dve", lambda e: e.tensor_tensor(out=g.x[:, t, :], in0=g.x[:, t, :], in1=ps_y[:, :],
                                                      op=ALU.add), reads=[ps_y_b, g.xb[t]], writes=[g.xb[t]])

            NR = NT * 8
            a_loads(0)
            a_qk(0)
            for n in range(NR):
                if n + 1 < NR:
                    if (n + 1) % 8 == 0:
                        a_loads((n + 1) // 8)
                    a_qk(n + 1)
                a_be(n)
                a_pv(n)
                if n % 2 == 1:
                    a_fin(n // 8, (n % 8) // 2)
                if n % 8 == 7:
                    a_outproj(n // 8)
        else:
            goffs = [0, 2176, 4736]
            acc = sb("acc", [65, 4, TPC], F32)
            acc_b = Buf()
            Kg = sb("Kg", [64, 4096], BF16)
            Kg_b = Buf()
            Vg = sb("Vg", [128, 32, 65], BF16)
            Vg_b = Buf()
            s_kv = P.dsem(f"s_kvk_{l}")
            s_kv2 = P.dsem(f"s_kvv_{l}")
            biasg = [sb(f"biasg{i}", [128, 1024], F32) for i in range(2)]
            biasg_b = [Buf() for _ in range(2)]
            biasg_s = [P.dsem(f"s_biasg{i}_{l}") for i in range(2)]
            QTsrc = dr["QT"].rearrange("(h d) s -> d h s", d=64)
            GTsrc = dr["GT"].rearrange("(h d) s -> d h s", d=64)
            cntc = dict(g=0, t=0)
            for kv in range(2):
                for gi, (_, dil) in enumerate(CG):
                    Ll = TPC // dil
                    ext = TPC + 128 * dil
                    ntile = 16 + dil
                    nti = Ll // 128
                    goff = goffs[gi]
                    bgi = cntc["g"] % 2
                    cntc["g"] += 1
                    P.dma("sp", Kg[:, 0:ext], dr["KX"][kv * 64:(kv + 1) * 64, goff:goff + ext], s_kv,
                          reads=[dr["KX_b"]], writes=[Kg_b])
                    P.dma("pool", Vg[:, 0:ntile, :],
                          dr["VX"][goff:goff + ext, kv * 65:(kv + 1) * 65].rearrange("(t p) c -> p t c", p=128), s_kv2,
                          reads=[dr["VX_b"]], writes=[Vg_b])
                    boff = (gi * 2 + kv) * 1024
                    P.dma("pool", biasg[bgi][:], dr["bias"][:, boff:boff + 1024], biasg_s[bgi], writes=[biasg_b[bgi]])
                    for r in range(dil):
                        for i_ in range(nti):
                            s0 = (r * nti + i_) * 128
                            tb = cntc["t"] % 2
                            cntc["t"] += 1
                            h0 = gi * 8 + kv * 4
                            P.dma("sp", QTt[tb][:, 0:4, :], QTsrc[:, h0:h0 + 4, s0:s0 + 128], QTt_s[tb],
                                  reads=[dr["QT_b"]], writes=[QTt_b[tb]])
                            rb = cnt["r"] % 2
                            cnt["r"] += 1
                            kbase = r * (Ll + 128) + 128 * i_
                            for gg in range(4):
                                for o in range(2):
                                    P.op("pe", lambda e, gg=gg, o=o: e.matmul(
                                        ps_S[rb][:, (gg * 2 + o) * 128:(gg * 2 + o + 1) * 128],
                                        lhsT=Kg[:, kbase + 128 * o:kbase + 128 * o + 128], rhs=QTt[tb][:, gg, :],
                                        start=True, stop=True), reads=[Kg_b, QTt_b[tb]], writes=[ps_S_b[rb]])
                            P.op("dve", lambda e: e.scalar_tensor_tensor(
                                out=T[rb][:, :], in0=ps_S[rb][:, :], scalar=0.125, in1=biasg[bgi][:, :],
                                op0=ALU.mult, op1=ALU.add), reads=[ps_S_b[rb], biasg_b[bgi]], writes=[T_b[rb]])
                            if i_ == 0 or i_ == nti - 1:
                                for o in range(2):
                                    ecol = 3 if (i_ == 0 and o == 0) else (4 if (i_ == nti - 1 and o == 1) else 0)
                                    P.op("act", lambda e, o=o, ecol=ecol: e.activation(
                                        out=Pm[rb][:, :].rearrange("p (g o q) -> p g o q", g=4, o=2)[:, :, o, :],
                                        in_=T[rb][:, :].rearrange("p (g o q) -> p g o q", g=4, o=2)[:, :, o, :],
                                        func=AF.Exp, bias=g.edge[:, ecol:ecol + 1]),
                                        reads=[T_b[rb], g.edgeb], writes=[Pm_b[rb]])
                            else:
                                P.op("act", lambda e: e.activation(out=Pm[rb][:, :], in_=T[rb][:, :], func=AF.Exp),
                                     reads=[T_b[rb]], writes=[Pm_b[rb]])
                            mt = r * (nti + 1) + i_
                            for gg in range(4):
                                for o in range(2):
                                    P.op("pe", lambda e, gg=gg, o=o: e.matmul(
                                        ps_O[0:65, gg * 128:(gg + 1) * 128], lhsT=Vg[:, mt + o, :],
                                        rhs=Pm[rb][:, (gg * 2 + o) * 128:(gg * 2 + o + 1) * 128],
                                        start=(o == 0), stop=(o == 1)), reads=[Vg_b, Pm_b[rb]], writes=[ps_O_b])
                            start = r + dil * 128 * i_
                            av = acc[0:65, :, start:start + dil * 127 + 1:dil]
                            pv = ps_O[0:65, 0:512].rearrange("p (g q) -> p g q", g=4)
                            if gi == 0:
                                P.op("dve", lambda e: e.tensor_copy(out=av, in_=pv), reads=[ps_O_b], writes=[acc_b])
                            else:
                                P.op("dve", lambda e: e.tensor_tensor(out=av, in0=av, in1=pv, op=ALU.add),
                                     reads=[ps_O_b, acc_b], writes=[acc_b])
                v4 = lambda ap: ap.rearrange("p (g q) -> p g q", g=4)
                for t in range(NT):
                    tb = t % 2
                    P.dma("sp", GTt[tb][:, 0:4, :], GTsrc[:, 4 * kv:4 * kv + 4, t * 128:(t + 1) * 128], GTt_s[tb],
                          reads=[dr["GT_b"]], writes=[GTt_b[tb]])
                    P.op("dve", lambda e, t=t: e.reciprocal(out=v4(den[64:65, 0:512]),
                                                           in_=acc[64:65, :, t * 128:(t + 1) * 128]),
                         reads=[acc_b], writes=[den_b])
                    P.op("pe", lambda e: e.matmul(ps_bc[0:64, 0:512], lhsT=ones[64:65, 0:64], rhs=den[64:65, 0:512],
                                                  start=True, stop=True), reads=[den_b, ones_b], writes=[ps_bc_b])
                    P.op("dve", lambda e, t=t, tb=tb: e.tensor_tensor(
                        out=v4(tmp[:, 0:512]), in0=acc[0:64, :, t * 128:(t + 1) * 128], in1=GTt[tb][:, 0:4, :],
                        op=ALU.mult), reads=[acc_b, GTt_b[tb]], writes=[tmp_b])
                    P.op("dve", lambda e, tb=tb: e.tensor_tensor(
                        out=OG[tb][:, 0:4, :], in0=v4(tmp[:, 0:512]), in1=v4(ps_bc[0:64, 0:512]), op=ALU.mult),
                        reads=[tmp_b, ps_bc_b], writes=[OG_b[tb]])
                    for cb in range(2):
                        for gg in range(4):
                            P.op("pe", lambda e, cb=cb, gg=gg, tb=tb: e.matmul(
                                ps_y[:, cb * 512:(cb + 1) * 512], lhsT=OG[tb][:, gg, :],
                                rhs=wo[:, 4 * kv + gg, cb * 512:(cb + 1) * 512], start=(gg == 0), stop=(gg == 3)),
                                reads=[OG_b[tb], wo_b[(4 * kv + gg) // 2]], writes=[ps_y_b])
                    P.op("dve", lambda e, t=t: e.tensor_tensor(out=g.x[:, t, :], in0=g.x[:, t, :], in1=ps_y[:, :],
                                                               op=ALU.add), reads=[ps_y_b, g.xb[t]], writes=[g.xb[t]])
        P.barrier()
    P.new_engine_sems(("pe",))


def _p1_drams(P, l, ext_out=True):
    kind = KIND[l]
    cf = lcfg(kind)
    nh_all = 20 if kind != 2 else 10
    mk = P.dout if ext_out else P.dint
    dr = dict(w_in=P.din(f"w_in_{l}", [DM, cf["W"]], F32), norm=P.din(f"norm_{l}", [128, 8], F32),
              gains=P.din(f"gains_{l}", [128, nh_all * 64], F32))
    if kind == 1:
        dr["rcos"] = P.din("rcos", [TPC, 32], F32)
        dr["rsin"] = P.din("rsin", [TPC, 32], F32)
    dr["QT"] = mk(f"QT_{l}", [cf["NH"] * 64, TPC], BF16)
    dr["GT"] = mk(f"GT_{l}", [cf["NS"] * 64, TPC], BF16)
    dr["EK"] = mk(f"EK_{l}", [cf["NKV"] * 64, TPC], BF16)
    dr["EV"] = mk(f"EV_{l}", [512, 1040] if kind == 1 else [TPC, cf["NKV"] * 65], BF16)
    for k in ("QT", "GT", "EK", "EV"):
        dr[k + "_b"] = Buf()
    return dr


def build_p1(l):
    P = Prog()
    g = setup_globals(P)
    d_x = P.din("x", [TPC, DM], F32)
    load_x(P, g, d_x)
    dr = _p1_drams(P, l)
    phase1(P, g, l, dr)
    P.finish()
    return P


def build_p2(l):
    kind = KIND[l]
    cf = lcfg(kind)
    P = Prog()
    g = setup_globals(P)
    d_x = P.din("x", [TPC, DM], F32)
    load_x(P, g, d_x)
    dr = dict(w_out=P.din(f"w_out_{l}", [cf["FO"], DM], F32),
              QT=P.din(f"QT_{l}", [cf["NH"] * 64, TPC], BF16), GT=P.din(f"GT_{l}", [cf["NS"] * 64, TPC], BF16))
    for k in ("QT", "GT", "KX", "VX", "GK", "GV"):
        dr[k + "_b"] = Buf()
    if kind == 0:
        dr["sink"] = P.din(f"sink_{l}", [1, 16], F32)
        dr["bias"] = P.din("BA", [128, 6144], F32)
        dr["KX"] = P.din("KX", [256, TPC + 256], BF16)
        dr["VX"] = P.din("VX", [TPC + 256, 260], BF16)
        phase2_banded(P, g, l, dr)
    elif kind == 2:
        dr["bias"] = P.din("BC", [128, 6144], F32)
        dr["KX"] = P.din("KX", [128, 8832], BF16)
        dr["VX"] = P.din("VX", [8832, 130], BF16)
        phase2_banded(P, g, l, dr)
    else:
        dr["GK"] = P.din("GK", [NCORES * 256, TPC], BF16)
        dr["GV"] = P.din("GV", [NCORES * 512, 1040], BF16)
        phase2_dense(P, g, l, dr)
    d_xo = P.dout("xo", [TPC, DM], F32)
    store_x(P, g, d_xo)
    P.finish()
    return P


def host_exchange(kind, EK, EV):
    out = [dict() for _ in range(NCORES)]
    if kind == 1:
        GK = np.concatenate(EK, axis=0)
        GV = np.concatenate(EV, axis=0)
        for c in range(NCORES):
            out[c]["GK"] = GK
            out[c]["GV"] = GV
        return out
    if kind == 0:
        for c in range(NCORES):
            zk = np.zeros((256, 128), NPBF)
            zv = np.zeros((128, 260), NPBF)
            kl = EK[c - 1][:, -128:] if c > 0 else zk
            kr = EK[c + 1][:, :128] if c < NCORES - 1 else zk
            vl = EV[c - 1][-128:, :] if c > 0 else zv
            vr = EV[c + 1][:128, :] if c < NCORES - 1 else zv
            out[c]["KX"] = np.ascontiguousarray(np.concatenate([kl, EK[c], kr], axis=1))
            out[c]["VX"] = np.ascontiguousarray(np.concatenate([vl, EV[c], vr], axis=0))
        return out
    for c in range(NCORES):
        kx, vx = [], []
        for gi, (_, dil) in enumerate(CG):
            Ll = TPC // dil
            for r in range(dil):
                rows = slice(gi * 128, gi * 128 + 128)
                cols = slice(gi * 130, gi * 130 + 130)
                zk = np.zeros((128, 64), NPBF)
                zv = np.zeros((64, 130), NPBF)
                kl = EK[c - 1][rows, r * Ll + Ll - 64:r * Ll + Ll] if c > 0 else zk
                kr = EK[c + 1][rows, r * Ll:r * Ll + 64] if c < NCORES - 1 else zk
                vl = EV[c - 1][r * Ll + Ll - 64:r * Ll + Ll, cols] if c > 0 else zv
                vr = EV[c + 1][r * Ll:r * Ll + 64, cols] if c < NCORES - 1 else zv
                kx += [kl, EK[c][rows, r * Ll:(r + 1) * Ll], kr]
                vx += [vl, EV[c][r * Ll:(r + 1) * Ll, cols], vr]
        out[c]["KX"] = np.ascontiguousarray(np.concatenate(kx, axis=1))
        out[c]["VX"] = np.ascontiguousarray(np.concatenate(vx, axis=0))
    return out


def run_prog(P, maps):
    in_maps = [{k: m[k] for k in P.in_names} for m in maps]
    res = run_bass_kernel_spmd(P.nc, in_maps, core_ids=list(range(NCORES)))
    return res.results


def kernel_unfused(inputs, nlayers=4):
    x = np.asarray(inputs["x"], np.float32)[0]
    xs = [np.ascontiguousarray(x[c * TPC:(c + 1) * TPC]) for c in range(NCORES)]
    consts = [host_consts(c) for c in range(NCORES)]
    for l in range(nlayers):
        kind = KIND[l]
        li = host_layer_inputs(inputs, l)
        maps = [dict(consts[c], **li, x=xs[c]) for c in range(NCORES)]
        r1 = run_prog(build_p1(l), maps)
        ex = host_exchange(kind, [r[f"EK_{l}"] for r in r1], [r[f"EV_{l}"] for r in r1])
        for c in range(NCORES):
            maps[c].update(ex[c])
            maps[c][f"QT_{l}"] = r1[c][f"QT_{l}"]
            maps[c][f"GT_{l}"] = r1[c][f"GT_{l}"]
        r2 = run_prog(build_p2(l), maps)
        xs = [r2[c]["xo"] for c in range(NCORES)]
    return np.concatenate(xs, axis=0)[None].astype(np.float32)


def phase2_dense(P, g, l, dr):
    nc = P.nc
    PH = 2
    with ExitStack() as st:
        def sb(name, shape, dt):
            return st.enter_context(nc.sbuf_tensor(f"sb{PH}_{name}_{l}", list(shape), dt))

        def ps(name, shape, dt):
            return st.enter_context(nc.psum_tensor(f"ps{PH}_{name}_{l}", list(shape), dt))

        wo, wo_b = load_wout(P, nc, st, l, dr, 16)
        ones = sb("ones", [65, 64], F32)
        ones_b = Buf()
        P.op("dve", lambda e: e.memset(ones[:], 1.0), writes=[ones_b])
        ps_S = [ps(f"ps_S{i}", [128, 1024], F32) for i in range(2)]
        ps_S_b = [Buf() for _ in range(2)]
        ps_O = [ps(f"ps_O{i}", [65, 512], F32) for i in range(4)]
        ps_O_b = [Buf() for _ in range(4)]
        Pm = [sb(f"Pm{i}", [128, 1024], BF16) for i in range(3)]
        Pm_b = [Buf() for _ in range(3)]
        NKB = 3
        Kc = [sb(f"Kc{i}", [128, TPC], BF16) for i in range(NKB)]
        Kc_b = [Buf() for _ in range(NKB)]
        Kc_s = [P.dsem(f"s_Kc{i}_{l}") for i in range(NKB)]
        Vc = [sb(f"Vc{i}", [128, 16, 65], BF16) for i in range(NKB)]
        Vc_b = [Buf() for _ in range(NKB)]
        Vc_s = [P.dsem(f"s_Vc{i}_{l}") for i in range(NKB)]
        QTb = [sb(f"QTb{i}", [128, 2, 512], BF16) for i in range(2)]
        QTb_b = [Buf() for _ in range(2)]
        QTb_s = [P.dsem(f"s_QTb{i}_{l}") for i in range(2)]
        GTb = [sb(f"GTb{i}", [64, 4, 512], BF16) for i in range(2)]
        GTb_b = [Buf() for _ in range(2)]
        GTb_s = [P.dsem(f"s_GTb{i}_{l}") for i in range(2)]
        OG = sb("OG", [64, 16, 512], BF16)
        OG_b = [Buf() for _ in range(4)]
        den = sb("den", [65, 512], F32)
        den_b = Buf()
        tmp = sb("tmp", [64, 512], F32)
        tmp_b = Buf()
        QTsrc = dr["QT"].rearrange("(h d) s -> d h s", d=64)
        GTsrc = dr["GT"].rearrange("(h d) s -> d h s", d=64)
        GVv = dr["GV"].rearrange("r (t e) -> r t e", e=65)
        cnt = dict(blk=0, ch=0, r=0)
        for qb in range(4):
            for j in range(4):
                bb = cnt["blk"] % 2
                cnt["blk"] += 1
                P.dma("pool", QTb[bb][:],
                      dr["QT"][4 * j * 64:(4 * j + 4) * 64, qb * 512:(qb + 1) * 512].rearrange("(hp p) s -> p hp s", p=128),
                      QTb_s[bb], reads=[dr["QT_b"]], writes=[QTb_b[bb]])
                P.dma("pool", GTb[bb][:], GTsrc[:, 4 * j:4 * j + 4, qb * 512:(qb + 1) * 512], GTb_s[bb],
                      reads=[dr["GT_b"]], writes=[GTb_b[bb]])
                pend = None

                def emit_pv(info):
                    rb, pb, cb_, kt, hp, first, last = info
                    for g2 in range(2):
                        gg = 2 * hp + g2
                        P.op("pe", lambda e, g2=g2, gg=gg: e.matmul(
                            ps_O[gg][0:65, 0:512], lhsT=Vc[cb_][:, kt, :], rhs=Pm[pb][:, g2 * 512:(g2 + 1) * 512],
                            start=first, stop=last), reads=[Vc_b[cb_], Pm_b[pb]], writes=[ps_O_b[gg]])

                for rk in range(NCORES):
                    cb_ = cnt["ch"] % NKB
                    cnt["ch"] += 1
                    P.dma("sp", Kc[cb_][0:64, :], dr["GK"][rk * 256 + j * 64:rk * 256 + (j + 1) * 64, :], Kc_s[cb_],
                          reads=[dr["GK_b"]], writes=[Kc_b[cb_]])
                    P.dma("pool", Kc[cb_][64:128, :], dr["GK"][rk * 256 + j * 64:rk * 256 + (j + 1) * 64, :], Kc_s[cb_],
                          reads=[dr["GK_b"]], writes=[Kc_b[cb_]])
                    P.dma("sp", Vc[cb_][:], GVv[rk * 512 + j * 128:rk * 512 + (j + 1) * 128, :, :], Vc_s[cb_],
                          reads=[dr["GV_b"]], writes=[Vc_b[cb_]])
                    for kt in range(16):
                        for hp in range(2):
                            rb = cnt["r"] % 2
                            pb = cnt["r"] % 3
                            cnt["r"] += 1
                            for g2 in range(2):
                                P.op("pe", lambda e, g2=g2: e.matmul(
                                    ps_S[rb][:, g2 * 512:(g2 + 1) * 512],
                                    lhsT=Kc[cb_][g2 * 64:(g2 + 1) * 64, kt * 128:(kt + 1) * 128],
                                    rhs=QTb[bb][g2 * 64:(g2 + 1) * 64, hp, :], start=True, stop=True),
                                    reads=[Kc_b[cb_], QTb_b[bb]], writes=[ps_S_b[rb]])
                            P.op("act", lambda e: e.activation(out=Pm[pb][:, :], in_=ps_S[rb][:, :], func=AF.Exp,
                                                               scale=0.125),
                                 reads=[ps_S_b[rb]], writes=[Pm_b[pb]])
                            if pend is not None:
                                emit_pv(pend)
                            pend = (rb, pb, cb_, kt, hp, (rk == 0 and kt == 0), (rk == NCORES - 1 and kt == 15))
                emit_pv(pend)
                for gg in range(4):
                    P.op("dve", lambda e, gg=gg: e.reciprocal(out=den[64:65, 0:512], in_=ps_O[gg][64:65, 0:512]),
                         reads=[ps_O_b[gg]], writes=[den_b])
                    rb = cnt["r"] % 2
                    cnt["r"] += 1
                    P.op("pe", lambda e, rb=rb: e.matmul(ps_S[rb][0:64, 0:512], lhsT=ones[64:65, 0:64],
                                                         rhs=den[64:65, 0:512], start=True, stop=True),
                         reads=[den_b, ones_b], writes=[ps_S_b[rb]])
                    P.op("dve", lambda e, gg=gg: e.tensor_tensor(out=tmp[:, :], in0=ps_O[gg][0:64, 0:512],
                                                                 in1=GTb[bb][:, gg, :], op=ALU.mult),
                         reads=[ps_O_b[gg], GTb_b[bb]], writes=[tmp_b])
                    P.op("dve", lambda e, gg=gg, rb=rb: e.tensor_tensor(out=OG[:, 4 * j + gg, :], in0=tmp[:, :],
                                                                        in1=ps_S[rb][0:64, 0:512], op=ALU.mult),
                         reads=[tmp_b, ps_S_b[rb]], writes=[OG_b[j]])
            for tt in range(4):
                t = qb * 4 + tt
                rb = cnt["r"] % 2
                cnt["r"] += 1
                for cb in range(2):
                    for h in range(16):
                        P.op("pe", lambda e, cb=cb, h=h, rb=rb: e.matmul(
                            ps_S[rb][:, cb * 512:(cb + 1) * 512], lhsT=OG[:, h, tt * 128:(tt + 1) * 128],
                            rhs=wo[:, h, cb * 512:(cb + 1) * 512], start=(h == 0), stop=(h == 15)),
                            reads=[OG_b[h // 4], wo_b[h // 2]], writes=[ps_S_b[rb]])
                P.op("dve", lambda e, t=t, rb=rb: e.tensor_tensor(out=g.x[:, t, :], in0=g.x[:, t, :], in1=ps_S[rb][:, :],
                                                                  op=ALU.add),
                     reads=[ps_S_b[rb], g.xb[t]], writes=[g.xb[t]])
        P.barrier()
    P.new_engine_sems(("pe",))


def device_exchange(P, g, l, dr, GK, GV, KX, VX, KX_b, VX_b, GK_b, GV_b):
    nc = P.nc
    kind = KIND[l]
    with ExitStack() as st:
        def sb(name, shape, dt):
            return st.enter_context(nc.sbuf_tensor(f"sbx_{name}_{l}", list(shape), dt))
        MAXF = 2080
        cand = [sb(f"cand{i}", [128, NCORES, MAXF], BF16) for i in range(2)]
        cand_b = [Buf() for _ in range(2)]
        cand_s = [P.dsem(f"s_cand{i}_{l}") for i in range(2)]
        acc = [sb(f"xacc{i}", [128, MAXF], F32) for i in range(2)]
        acc_b = [Buf() for _ in range(2)]
        res = [sb(f"xres{i}", [128, MAXF], BF16) for i in range(2)]
        res_b = [Buf() for _ in range(2)]
        res_s = [P.dsem(f"s_xres{i}_{l}") for i in range(2)]
        sel = sb("sel", [128, 16], F32)
        sel_b = Buf()
        s_own = P.dsem(f"s_xownk_{l}")
        s_own2 = P.dsem(f"s_xownv_{l}")
        s_sel = P.dsem(f"s_xsel_{l}")
        P.dma("sp", sel[:], dr["sel"][:, :], s_sel, writes=[sel_b])
        cnt = dict(i=0)

        def select(np_, shape3, src_ap, dst_ap, side, src_b, dst_b):
            i = cnt["i"] % 2
            cnt["i"] += 1
            F = int(np.prod(shape3))
            names = "abc"[:len(shape3)]
            pat = "p (" + " ".join(names) + ") -> p " + " ".join(names)
            kw = {n: v for n, v in zip(names, shape3)}
            cv = lambda r: cand[i][0:np_, r, 0:F].rearrange(pat, **kw) if len(shape3) > 1 else cand[i][0:np_, r, 0:F]
            for r in range(NCORES):
                P.dma("sp" if r % 2 == 0 else "pool", cv(r), src_ap[:, r], cand_s[i], reads=[src_b],
                      writes=[cand_b[i]])
            eng = "dve"
            w0 = side * 8
            P.op(eng, lambda e: e.tensor_scalar(out=acc[i][0:np_, 0:F], in0=cand[i][0:np_, 0, 0:F],
                                                scalar1=sel[0:np_, w0:w0 + 1], scalar2=None, op0=ALU.mult),
                 reads=[cand_b[i], sel_b], writes=[acc_b[i]])
            for r in range(1, NCORES - 1):
                P.op(eng, lambda e, r=r: e.scalar_tensor_tensor(
                    out=acc[i][0:np_, 0:F], in0=cand[i][0:np_, r, 0:F], scalar=sel[0:np_, w0 + r:w0 + r + 1],
                    in1=acc[i][0:np_, 0:F], op0=ALU.mult, op1=ALU.add), reads=[cand_b[i], sel_b, acc_b[i]],
                    writes=[acc_b[i]])
            r = NCORES - 1
            P.op(eng, lambda e: e.scalar_tensor_tensor(
                out=res[i][0:np_, 0:F], in0=cand[i][0:np_, r, 0:F], scalar=sel[0:np_, w0 + r:w0 + r + 1],
                in1=acc[i][0:np_, 0:F], op0=ALU.mult, op1=ALU.add), reads=[cand_b[i], sel_b, acc_b[i]],
                writes=[res_b[i]])
            rv = res[i][0:np_, 0:F].rearrange(pat, **kw) if len(shape3) > 1 else res[i][0:np_, 0:F]
            P.dma("sp" if i == 0 else "pool", dst_ap, rv, res_s[i], reads=[res_b[i]], writes=[dst_b])

        if kind == 0:
            P.dma("sp", KX[:, 128:128 + TPC], dr["EK"][:, :], s_own, reads=[dr["EK_b"]], writes=[KX_b])
            P.dma("pool", VX[128:128 + TPC, :], dr["EV"][:, :], s_own2, reads=[dr["EV_b"]], writes=[VX_b])
            GKv = GK.rearrange("(rk rc p) s -> p rk rc s", rk=NCORES, rc=2, p=128)
            KXv = KX.rearrange("(rc p) s -> p rc s", p=128)
            select(128, (2, 128), GKv[:, :, :, TPC - 128:TPC], KXv[:, :, 0:128], 0, GK_b, KX_b)
            select(128, (2, 128), GKv[:, :, :, 0:128], KXv[:, :, 128 + TPC:256 + TPC], 1, GK_b, KX_b)
            GVv = GV.rearrange("(rk s) c -> s rk c", rk=NCORES)
            select(128, (260,), GVv[TPC - 128:TPC, :, :], VX[0:128, :], 0, GV_b, VX_b)
            select(128, (260,), GVv[0:128, :, :], VX[128 + TPC:256 + TPC, :], 1, GV_b, VX_b)
        else:
            goffs = [0, 2176, 4736]
            for gi, (_, dil) in enumerate(CG):
                Ll = TPC // dil
                ext = TPC + 128 * dil
                goff = goffs[gi]
                KXg = KX[:, goff:goff + ext].rearrange("row (cl e) -> row cl e", cl=dil)
                VXg = VX[goff:goff + ext, :].rearrange("(cl e) c -> e cl c", cl=dil)
                P.dma("sp", KXg[:, :, 64:64 + Ll],
                      dr["EK"][gi * 128:(gi + 1) * 128, :].rearrange("row (cl u) -> row cl u", cl=dil), s_own,
                      reads=[dr["EK_b"]], writes=[KX_b])
                P.dma("pool", VX[goff:goff + ext, :].rearrange("(cl e) c -> cl e c", cl=dil)[:, 64:64 + Ll, :],
                      dr["EV"][:, gi * 130:(gi + 1) * 130].rearrange("(cl u) c -> cl u c", cl=dil), s_own2,
                      reads=[dr["EV_b"]], writes=[VX_b])
                GKv = GK.rearrange("(rk row) (cl u) -> row rk cl u", rk=NCORES, cl=dil)[gi * 128:(gi + 1) * 128]
                select(128, (dil, 64), GKv[:, :, :, Ll - 64:Ll], KXg[:, :, 0:64], 0, GK_b, KX_b)
                select(128, (dil, 64), GKv[:, :, :, 0:64], KXg[:, :, 64 + Ll:128 + Ll], 1, GK_b, KX_b)
                GVv = GV.rearrange("(rk cl u) c -> u rk cl c", rk=NCORES, cl=dil)[:, :, :, gi * 130:(gi + 1) * 130]
                select(64, (dil, 130), GVv[Ll - 64:Ll], VXg[0:64, :, :], 0, GV_b, VX_b)
                select(64, (dil, 130), GVv[0:64], VXg[64 + Ll:128 + Ll, :, :], 1, GV_b, VX_b)
        P.barrier()


def build_fused(nlayers=4, layers=None):
    P = Prog()
    g = setup_globals(P)
    d_x = P.din("x", [TPC, DM], F32)
    load_x(P, g, d_x)
    d_sel = P.din("sel", [128, 16], F32)
    s_cc = P.dsem("s_cc")
    s_cc2 = P.dsem("s_ccb")
    s_cc3 = P.dsem("s_ccc")
    fence_in = P.dint("fence_in", [128, 64], F32)
    fence_out = P.dint("fence_out", [NCORES * 128, 64], F32)
    for l in (layers if layers is not None else range(nlayers)):
        kind = KIND[l]
        cf = lcfg(kind)
        dr = _p1_drams(P, l, ext_out=False)
        phase1(P, g, l, dr)
        evshape = [512, 1040] if kind == 1 else [TPC, cf["NKV"] * 65]
        GK = P.dint(f"GK_{l}", [NCORES * cf["NKV"] * 64, TPC], BF16)
        GV = P.dint(f"GV_{l}", [NCORES * evshape[0], evshape[1]], BF16)
        GK_b, GV_b = Buf(), Buf()
        P.coll(dr["EK"], GK, P.sem(f"cc_k_{l}"), reads=[dr["EK_b"]], writes=[GK_b])
        P.coll(dr["EV"], GV, P.sem(f"cc_v_{l}"), reads=[dr["EV_b"]], writes=[GV_b])
        P.coll(fence_in, fence_out, P.sem(f"cc_f_{l}"), reads=[GK_b, GV_b], writes=[GK_b, GV_b])
        dr2 = dict(w_out=P.din(f"w_out_{l}", [cf["FO"], DM], F32), QT=dr["QT"], GT=dr["GT"],
                   QT_b=dr["QT_b"], GT_b=dr["GT_b"], sel=d_sel, EK=dr["EK"], EV=dr["EV"], EK_b=dr["EK_b"],
                   EV_b=dr["EV_b"])
        if kind == 1:
            dr2.update(GK=GK, GV=GV, GK_b=GK_b, GV_b=GV_b)
            phase2_dense(P, g, l, dr2)
        else:
            kxs = [256, TPC + 256] if kind == 0 else [128, 8832]
            vxs = [TPC + 256, 260] if kind == 0 else [8832, 130]
            KX = P.dint(f"KX_{l}", kxs, BF16)
            VX = P.dint(f"VX_{l}", vxs, BF16)
            KX_b, VX_b = Buf(), Buf()
            device_exchange(P, g, l, dr2, GK, GV, KX, VX, KX_b, VX_b, GK_b, GV_b)
            dr2.update(KX=KX, VX=VX, KX_b=KX_b, VX_b=VX_b)
            if kind == 0:
                dr2["sink"] = P.din(f"sink_{l}", [1, 16], F32)
                dr2["bias"] = P.din("BA", [128, 6144], F32)
            else:
                dr2["bias"] = P.din("BC", [128, 6144], F32)
            phase2_banded(P, g, l, dr2)
    d_xo = P.dout("xo", [TPC, DM], F32)
    store_x(P, g, d_xo)
    P.finish()
    return P


def host_sel(c):
    s = np.zeros((128, 16), np.float32)
    if c > 0:
        s[:, c - 1] = 1.0
    if c < NCORES - 1:
        s[:, 8 + c + 1] = 1.0
    return s


def kernel_fused(inputs, nlayers=4, layers=None, x0=None):
    x = np.asarray(inputs["x"], np.float32)[0] if x0 is None else x0
    maps = []
    lis = {}
    for l in (layers if layers is not None else range(nlayers)):
        lis.update(host_layer_inputs(inputs, l))
    for c in range(NCORES):
        m = dict(host_consts(c), **lis)
        m["x"] = np.ascontiguousarray(x[c * TPC:(c + 1) * TPC])
        m["sel"] = host_sel(c)
        maps.append(m)
    P = build_fused(nlayers, layers)
    r = run_prog(P, maps)
    return np.concatenate([r[c]["xo"] for c in range(NCORES)], axis=0)[None].astype(np.float32)


def kernel(**inputs):
    if FUSED:
        return kernel_fused(inputs)
    return kernel_unfused(inputs)
```
